# Optimizing a Trainium2 kernel written in Bass

```python
import math
import jax, jax.numpy as jnp
from jax import lax
import numpy as np

D_MODEL = 2048
BATCH = 2
SEQ = 16384
DEPTH = 4

MLA_HEADS = 8
Q_LORA = 512
KV_LORA = 512
QK_NOPE = 128
QK_ROPE = 64
V_DIM = 128
ROPE_THETA = 10000.0
Q_BLOCK = 128
DIL_HEADS = 8
DIL_DIM = 128
DIL_PATTERNS = ((128, 1), (512, 4), (2048, 16))
DIL_BLOCK = 64
N_BRANCH = 2
D_FF = -(-8 * D_MODEL // (3 * 256)) * 256
RMS_EPS = 1e-6
NEG_INF = -1e30
D_IN = Q_LORA + KV_LORA + QK_ROPE + 3 * DIL_HEADS * DIL_DIM + N_BRANCH * D_MODEL

kernel_name = "hybrid_mla_dilated_gated_encoder"


def rms_norm(x, g):
    xf = x.astype(jnp.float32)
    y = xf * lax.rsqrt(jnp.mean(xf * xf, axis=-1, keepdims=True) + RMS_EPS)
    return (y * g.astype(jnp.float32)).astype(x.dtype)


def rope_tables(seq):
    inv = ROPE_THETA ** (-jnp.arange(0, QK_ROPE, 2, dtype=jnp.float32) / QK_ROPE)
    ang = jnp.arange(seq, dtype=jnp.float32)[:, None] * inv[None, :]
    return jnp.cos(ang), jnp.sin(ang)


def apply_rope(x, cos, sin):
    x1, x2 = jnp.split(x, 2, axis=-1)
    c = cos.astype(x.dtype)
    s = sin.astype(x.dtype)
    return jnp.concatenate([x1 * c - x2 * s, x1 * s + x2 * c], axis=-1)


def alibi_slopes(n_heads):
    return jnp.power(2.0, -8.0 * jnp.arange(1, n_heads + 1, dtype=jnp.float32) / n_heads)


def mla_attention(q_nope, q_rope, k_nope, k_rope, v):
    B, S, H, _ = q_nope.shape
    nq = S // Q_BLOCK
    scale = (QK_NOPE + QK_ROPE) ** -0.5

    def to_blocks(t):
        return jnp.moveaxis(t.reshape(B, nq, Q_BLOCK, *t.shape[2:]), 1, 0)

    def attend(blk):
        qn, qr = blk
        s = (jnp.einsum('bqhd,bkhd->bhqk', qn, k_nope, preferred_element_type=jnp.float32)
             + jnp.einsum('bqhd,bkd->bhqk', qr, k_rope, preferred_element_type=jnp.float32)) * scale
        p = jax.nn.softmax(s, axis=-1).astype(v.dtype)
        return jnp.einsum('bhqk,bkhd->bqhd', p, v)

    o = lax.map(attend, (to_blocks(q_nope), to_blocks(q_rope)))
    return jnp.moveaxis(o, 0, 1).reshape(B, S, H, v.shape[-1])


def strided_band_attention(q, k, v, slopes, n_side, dil):
    B, S, H, D = q.shape
    L = S // dil
    nb = -(-L // DIL_BLOCK)
    Lp = nb * DIL_BLOCK

    def to_sub(t):
        return t.reshape(B, L, dil, H, D).transpose(0, 2, 1, 3, 4)

    qs = jnp.pad(to_sub(q), ((0, 0), (0, 0), (0, Lp - L), (0, 0), (0, 0)))
    qs = qs.reshape(B, dil, nb, DIL_BLOCK, H, D)

    def windows(t):
        tp = jnp.pad(to_sub(t), ((0, 0), (0, 0), (DIL_BLOCK, DIL_BLOCK + Lp - L), (0, 0), (0, 0)))
        tb = tp.reshape(B, dil, nb + 2, DIL_BLOCK, H, D)
        return jnp.concatenate([tb[:, :, :-2], tb[:, :, 1:-1], tb[:, :, 2:]], axis=3)

    kw = windows(k)
    vw = windows(v)

    qi = jnp.arange(DIL_BLOCK)[:, None]
    kj = jnp.arange(3 * DIL_BLOCK)[None, :]
    delta = kj - DIL_BLOCK - qi
    kpos = (jnp.arange(nb)[:, None] - 1) * DIL_BLOCK + jnp.arange(3 * DIL_BLOCK)[None, :]
    valid = (jnp.abs(delta) <= n_side)[None] & ((kpos >= 0) & (kpos < L))[:, None, :]
    bias = -slopes[:, None, None] * (dil * jnp.abs(delta)).astype(jnp.float32)[None]

    s = jnp.einsum('bcnqhd,bcnkhd->bcnhqk', qs, kw, preferred_element_type=jnp.float32) * (D ** -0.5)
    s = jnp.where(valid[:, None], s + bias, NEG_INF)
    m = jnp.max(s, axis=-1, keepdims=True)
    p = jnp.exp(s - m)
    l = jnp.sum(p, axis=-1)
    o = jnp.einsum('bcnhqk,bcnkhd->bcnqhd', p.astype(v.dtype), vw).astype(jnp.float32)
    o = o / jnp.swapaxes(l, -1, -2)[..., None]
    lse = jnp.swapaxes(m[..., 0] + jnp.log(l), -1, -2)

    o = o.reshape(B, dil, Lp, H, D)[:, :, :L].transpose(0, 2, 1, 3, 4).reshape(B, S, H, D)
    lse = lse.reshape(B, dil, Lp, H)[:, :, :L].transpose(0, 2, 1, 3).reshape(B, S, H)
    return o, lse


def dilated_attention(q, k, v):
    slopes = alibi_slopes(q.shape[2])
    outs, lses = [], []
    for window, dil in DIL_PATTERNS:
        o, lse = strided_band_attention(q, k, v, slopes, window // (2 * dil), dil)
        outs.append(o)
        lses.append(lse)
    w = jax.nn.softmax(jnp.stack(lses), axis=0)
    o = sum(w[g][..., None] * outs[g] for g in range(len(DIL_PATTERNS)))
    return o.astype(v.dtype)


def hybrid_layer(x, cos, sin, g_mix, w_in, b_gate, g_cq, w_uq, g_ckv, w_ukv,
                 g_q_nope, g_q_rope, g_k_nope, g_k_rope, g_q_dil, g_k_dil,
                 w_o_mla, w_o_dil, w_o, g_ffn, w_gate_up, w_down):
    B, S, _ = x.shape
    h = rms_norm(x, g_mix)
    proj = h @ w_in
    dh = DIL_HEADS * DIL_DIM
    cuts = np.cumsum([Q_LORA, KV_LORA, QK_ROPE, dh, dh, dh]).tolist()
    c_q, c_kv, k_r, q_d, k_d, v_d, gate = jnp.split(proj, cuts, axis=-1)

    q = (rms_norm(c_q, g_cq) @ w_uq).reshape(B, S, MLA_HEADS, QK_NOPE + QK_ROPE)
    q_nope, q_rope = jnp.split(q, [QK_NOPE], axis=-1)
    q_nope = rms_norm(q_nope, g_q_nope)
    q_rope = apply_rope(rms_norm(q_rope, g_q_rope), cos[:, None, :], sin[:, None, :])
    kv = (rms_norm(c_kv, g_ckv) @ w_ukv).reshape(B, S, MLA_HEADS, QK_NOPE + V_DIM)
    k_nope, v_a = jnp.split(kv, [QK_NOPE], axis=-1)
    k_nope = rms_norm(k_nope, g_k_nope)
    k_rope = apply_rope(rms_norm(k_r, g_k_rope), cos, sin)
    o_a = mla_attention(q_nope, q_rope, k_nope, k_rope, v_a)
    y_a = o_a.reshape(B, S, MLA_HEADS * V_DIM) @ w_o_mla

    qd = rms_norm(q_d.reshape(B, S, DIL_HEADS, DIL_DIM), g_q_dil)
    kd = rms_norm(k_d.reshape(B, S, DIL_HEADS, DIL_DIM), g_k_dil)
    vd = v_d.reshape(B, S, DIL_HEADS, DIL_DIM)
    o_b = dilated_attention(qd, kd, vd)
    y_b = o_b.reshape(B, S, dh) @ w_o_dil

    g = jax.nn.sigmoid((gate + b_gate).astype(jnp.float32)).astype(x.dtype)
    g = g.reshape(B, S, N_BRANCH, D_MODEL)
    x = x + (g[:, :, 0] * y_a + g[:, :, 1] * y_b) @ w_o

    gt, up = jnp.split(rms_norm(x, g_ffn) @ w_gate_up, 2, axis=-1)
    return x + (jax.nn.silu(gt) * up) @ w_down


def setup_inputs(seed: int = 0) -> dict:
    key = jax.random.key(seed)
    ks = jax.random.split(key, 22)

    def dense(k, fan_in, fan_out):
        return jax.random.normal(k, (DEPTH, fan_in, fan_out), jnp.float32) * fan_in ** -0.5

    def gain(k, n):
        return 1.0 + 0.05 * jax.random.normal(k, (DEPTH, n), jnp.float32)

    return {
        "x": jax.random.normal(ks[0], (BATCH, SEQ, D_MODEL), jnp.float32),
        "g_mix": gain(ks[1], D_MODEL),
        "w_in": dense(ks[2], D_MODEL, D_IN),
        "b_gate": 0.02 * jax.random.normal(ks[3], (DEPTH, N_BRANCH * D_MODEL), jnp.float32),
        "g_cq": gain(ks[4], Q_LORA),
        "w_uq": dense(ks[5], Q_LORA, MLA_HEADS * (QK_NOPE + QK_ROPE)),
        "g_ckv": gain(ks[6], KV_LORA),
        "w_ukv": dense(ks[7], KV_LORA, MLA_HEADS * (QK_NOPE + V_DIM)),
        "g_q_nope": gain(ks[8], QK_NOPE),
        "g_q_rope": gain(ks[9], QK_ROPE),
        "g_k_nope": gain(ks[10], QK_NOPE),
        "g_k_rope": gain(ks[11], QK_ROPE),
        "g_q_dil": gain(ks[12], DIL_DIM),
        "g_k_dil": gain(ks[13], DIL_DIM),
        "w_o_mla": dense(ks[14], MLA_HEADS * V_DIM, D_MODEL),
        "w_o_dil": dense(ks[15], DIL_HEADS * DIL_DIM, D_MODEL),
        "w_o": dense(ks[16], D_MODEL, D_MODEL),
        "g_ffn": gain(ks[17], D_MODEL),
        "w_gate_up": dense(ks[18], D_MODEL, 2 * D_FF),
        "w_down": dense(ks[19], D_FF, D_MODEL),
    }


def reference(x, g_mix, w_in, b_gate, g_cq, w_uq, g_ckv, w_ukv,
              g_q_nope, g_q_rope, g_k_nope, g_k_rope, g_q_dil, g_k_dil,
              w_o_mla, w_o_dil, w_o, g_ffn, w_gate_up, w_down):
    cos, sin = rope_tables(x.shape[1])
    for i in range(DEPTH):
        x = hybrid_layer(x, cos, sin, g_mix[i], w_in[i], b_gate[i], g_cq[i], w_uq[i],
                         g_ckv[i], w_ukv[i], g_q_nope[i], g_q_rope[i], g_k_nope[i],
                         g_k_rope[i], g_q_dil[i], g_k_dil[i], w_o_mla[i], w_o_dil[i],
                         w_o[i], g_ffn[i], w_gate_up[i], w_down[i])
    return x
```

```python
import contextlib
import numpy as np
import ml_dtypes
import concourse.bass as bass
import concourse.mybir as mybir
from concourse.bass_utils import run_bass_kernel_spmd

F32 = mybir.dt.float32
BF16 = mybir.dt.bfloat16
AF = mybir.ActivationFunctionType
ALU = mybir.AluOpType

D = 2048
D_IN = 8256
D_FF = 5632
NCOL = 78
EPS = 1e-6
SC_A = 192 ** -0.5
SC_D = 128 ** -0.5
NREL = 20


class Buf:
    _n = 0

    def __init__(self, t=None, dram=False, key=None):
        self.t = t
        self.w = {}
        self.r = {}
        self.dram = dram
        Buf._n += 1
        self.key = key or ("b%d" % Buf._n)


class Rot:
    def __init__(self, items):
        self.items = items
        self.i = 0

    def next(self):
        b = self.items[self.i % len(self.items)]
        self.i += 1
        return b


class K:
    def __init__(self, nc, es):
        self.nc = nc
        self.es = es
        self.E = dict(pe=nc.tensor, act=nc.scalar, dve=nc.vector, pool=nc.gpsimd, sp=nc.sync)
        self.sem = {e: es.enter_context(nc.semaphore("s_" + e)) for e in self.E}
        self.cnt = {e: 0 for e in self.E}
        self.waited = {e: {} for e in self.E}
        self.dsem = {}
        self.dcnt = {}

    def _sync(self, e, reads, writes):
        deps = {}

        def merge(d):
            for k, (h, v) in d.items():
                if k not in deps or deps[k][1] < v:
                    deps[k] = (h, v)
        for b in reads:
            merge(b.w)
        for b in writes:
            if not b.dram:
                merge(b.w)
            merge(b.r)
        for k, (h, v) in deps.items():
            if e == "pe" and k == "pe":
                continue
            if self.waited[e].get(k, 0) >= v:
                continue
            self.E[e].wait_ge(h, v)
            self.waited[e][k] = v

    def op(self, e, fn, reads=(), writes=()):
        self._sync(e, reads, writes)
        ins = fn(self.E[e])
        self.cnt[e] += 1
        ins.then_inc(self.sem[e], 1)
        ev = (self.sem[e], self.cnt[e])
        for b in reads:
            b.r[e] = ev
        for b in writes:
            b.w[e] = ev

    def dma(self, q, out_ap, in_ap, reads, writes, owner):
        self._sync(q, reads, writes)
        ins = self.E[q].dma_start(out=out_ap, in_=in_ap)
        if owner.key not in self.dsem:
            self.dsem[owner.key] = self.es.enter_context(self.nc.semaphore("d_" + owner.key))
            self.dcnt[owner.key] = 0
        self.dcnt[owner.key] += 16
        ins.then_inc(self.dsem[owner.key], 16)
        ev = (self.dsem[owner.key], self.dcnt[owner.key])
        for b in reads:
            b.r[owner.key] = ev
        for b in writes:
            b.w[owner.key] = ev

    def final_wait(self, e, bufs):
        self._sync(e, bufs, [])

    def barrier(self, bufs):
        deps = {}
        for b in bufs:
            for d in (b.w, b.r):
                for kk, (h, v) in d.items():
                    if kk not in deps or deps[kk][1] < v:
                        deps[kk] = (h, v)
        for e in self.E:
            for kk, (h, v) in deps.items():
                if self.waited[e].get(kk, 0) >= v:
                    continue
                self.E[e].wait_ge(h, v)
                self.waited[e][kk] = v


def build(S, DEPTH):
    assert S % 512 == 0
    NCH = S // 512
    NKT = S // 128
    nc = bass.Bass("TRN2", target_bir_lowering=False)

    def din(name, shape, dt=F32):
        return nc.dram_tensor(name, shape, dt, kind="ExternalInput").ap()

    def dsc(name, shape, dt=BF16):
        return nc.dram_tensor(name, shape, dt).ap()

    x_in = din("x", [S, D])
    wsrc = dict(
        w_in=din("w_in", [DEPTH, D, D_IN]), w_uq=din("w_uq", [DEPTH, 512, 1536]),
        w_ukv=din("w_ukv", [DEPTH, 512, 2048]), w_o_mla=din("w_o_mla", [DEPTH, 1024, D]),
        w_o_dil=din("w_o_dil", [DEPTH, 1024, D]), w_o=din("w_o", [DEPTH, D, D]),
        w_gate_up=din("w_gate_up", [DEPTH, D, 2 * D_FF]), w_down=din("w_down", [DEPTH, D_FF, D]),
    )
    cols_in = din("cols", [DEPTH, 128, NCOL])
    cs_in = din("cs", [2, 64, S])
    wtab_in = din("wtab", [8, 128, NREL, 512], BF16)
    ident_in = din("ident", [128, 128], BF16)
    rmat_in = din("rmat", [64, 64], BF16)
    y = nc.dram_tensor("y", [S, D], F32, kind="ExternalOutput").ap()

    wb = {k: dsc("b_" + k, list(v.shape)) for k, v in wsrc.items()}
    wbuf = {k: Buf(dram=True) for k in wsrc}
    qnT = dsc("qnT", [8, 128, S]); qrT = dsc("qrT", [8, 64, S])
    knT = dsc("knT", [8, 128, S]); krT = dsc("krT", [64, S])
    vA = dsc("vA", [S, 1024]); qdT = dsc("qdT", [8, 128, S]); kdT = dsc("kdT", [8, 128, S])
    vD = dsc("vD", [S, 1024]); gT = dsc("gT", [32, 128, S])
    oaT = dsc("oaT", [8, 128, S]); obT = dsc("obT", [8, 128, S])
    B_qn, B_qr, B_kn, B_kr, B_vA, B_qd, B_kd, B_vD, B_g, B_oa, B_ob = [Buf(dram=True) for _ in range(11)]
    B_y = [Buf(dram=True) for _ in range(NCH)]
    B_const = Buf(dram=True)

    with contextlib.ExitStack() as es:
        k = K(nc, es)

        uid = [0]
        phase_bufs = []

        def sb(name, shape, dt, stack=es):
            uid[0] += 1
            b = Buf(stack.enter_context(nc.sbuf_tensor("%s_%d" % (name, uid[0]), shape, dt)), key=name)
            phase_bufs.append(b)
            return b

        def end_phase():
            k.barrier(phase_bufs + PS + [PT])
            del phase_bufs[:]

        def pst(name, shape, dt):
            return Buf(es.enter_context(nc.psum_tensor(name, shape, dt)))

        PS = [pst("ps%d" % i, [128, 512], F32) for i in range(7)]
        PT = pst("pt", [128, 1024], BF16)

        ones = sb("ones", [128, 128], BF16)
        ident = sb("ident", [128, 128], BF16)
        rmat = sb("rmat", [64, 64], BF16)
        cols = sb("cols", [128, NCOL], F32)
        k.op("dve", lambda e: e.memset(ones.t[:], 1.0), [], [ones])
        k.dma("sp", ident.t[:], ident_in, [B_const], [ident], ident)
        k.dma("sp", rmat.t[:], rmat_in, [B_const], [rmat], rmat)

        castown = Buf(key="cast")
        for l in range(DEPTH):
            for name, src in wsrc.items():
                rows = src.shape[1]
                for r0 in range(0, rows, 128):
                    k.dma("pool", wb[name][l, r0:r0 + 128, :], src[l, r0:r0 + 128, :], [B_const], [wbuf[name]], castown)
        for name in wsrc:
            wbuf[name].w[castown.key] = (k.dsem[castown.key], k.dcnt[castown.key])

        def wblock_load(dst, name, l, r0, nk, c0, ncol):
            for k0 in range(0, nk, 4):
                k1 = min(nk, k0 + 4)
                src = wb[name][l, r0 + k0 * 128:r0 + k1 * 128, c0:c0 + ncol].rearrange("(kt p) c -> p kt c", p=128)
                k.dma("sp", dst.t[:, k0:k1, 0:ncol], src, [wbuf[name]], [dst], dst)

        for l in range(DEPTH):
            xsrc = x_in if l == 0 else y
            k.dma("sp", cols.t[:], cols_in[l], [B_const], [cols], cols)

            def col(j, np_=128):
                return cols.t[0:np_, j:j + 1]

            with contextlib.ExitStack() as ph:
                xt = Rot([sb("a_x%d" % i, [128, D], F32, ph) for i in range(2)])
                hb = Rot([sb("a_hb%d" % i, [128, D], BF16, ph) for i in range(2)])
                ssq = Rot([sb("a_ss%d" % i, [128, 2], F32, ph) for i in range(2)])
                hT = sb("a_hT", [128, 16, 512], BF16, ph)
                wsl = Rot([sb("a_w%d" % i, [128, 16, 512], BF16, ph) for i in range(3)])
                wuq = sb("a_wuq", [128, 4, 1536], BF16, ph)
                wukv = sb("a_wukv", [128, 4, 2048], BF16, ph)
                vs = Rot([sb("a_vs%d" % i, [128, 4, 512], F32, ph) for i in range(2)])
                sq = Rot([sb("a_sq%d" % i, [128, 4, 512], BF16, ph) for i in range(2)])
                rr = Rot([sb("a_rr%d" % i, [128, 512], F32, ph) for i in range(2)])
                cqn = sb("a_cqn", [128, 4, 512], BF16, ph)
                ckvn = sb("a_ckvn", [128, 4, 512], BF16, ph)
                rp = Rot([sb("a_rp%d" % i, [64, 3, 512], F32, ph) for i in range(2)])
                rpb = Rot([sb("a_rpb%d" % i, [64, 512], BF16, ph) for i in range(2)])
                cst = Rot([sb("a_cs%d" % i, [64, 2, 512], F32, ph) for i in range(2)])
                stg = Rot([sb("a_st%d" % i, [128, 1024], BF16, ph) for i in range(6)])
                pmm = Rot(PS[0:4])
                pnm = Rot(PS[4:6])
                prt = PS[6]

                wblock_load(wuq, "w_uq", l, 0, 4, 0, 1536)
                wblock_load(wukv, "w_ukv", l, 0, 4, 0, 2048)

                def store(dst_ap, dbuf, st, np_, ncol):
                    k.dma("pool", dst_ap, st.t[0:np_, 0:ncol], [st], [dbuf], st)

                def rms_norm_fm(pss, np_, nfeat, gcols, outs):
                    T = len(pss)
                    v = vs.next(); s2 = sq.next(); r = rr.next(); pn = pnm.next()
                    for t, p in enumerate(pss):
                        k.op("act", lambda e, t=t, p=p: e.activation(out=v.t[0:np_, t, :], in_=p.t[0:np_, :], func=AF.Copy), [p], [v])
                        k.op("dve", lambda e, t=t: e.tensor_tensor(out=s2.t[0:np_, t, :], in0=v.t[0:np_, t, :], in1=v.t[0:np_, t, :], op=ALU.mult), [v], [s2])
                    for t in range(T):
                        k.op("pe", lambda e, t=t: e.matmul(pn.t[0:np_, :], ones.t[0:np_, 0:np_], s2.t[0:np_, t, :], start=(t == 0), stop=(t == T - 1)), [ones, s2], [pn])
                    k.op("act", lambda e: e.activation(out=r.t[0:np_, :], in_=pn.t[0:np_, :], func=AF.Sqrt, bias=EPS, scale=1.0 / nfeat), [pn], [r])
                    k.op("dve", lambda e: e.reciprocal(out=r.t[0:np_, :], in_=r.t[0:np_, :]), [r], [r])
                    for t in range(T):
                        ob, oap = outs[t]
                        k.op("dve", lambda e, t=t, oap=oap: e.scalar_tensor_tensor(out=oap, in0=v.t[0:np_, t, :], scalar=gcols[t], in1=r.t[0:np_, :], op0=ALU.mult, op1=ALU.mult), [v, r, cols], [ob])

                def rope(src_bf, csb, dst, dst_ap):
                    t = rp.next()
                    k.op("pe", lambda e: e.matmul(prt.t[0:64, :], rmat.t[:], src_bf.t[:], start=True, stop=True), [rmat, src_bf], [prt])
                    k.op("dve", lambda e: e.tensor_tensor(out=t.t[:, 0, :], in0=src_bf.t[:], in1=csb.t[:, 0, :], op=ALU.mult), [src_bf, csb], [t])
                    k.op("dve", lambda e: e.tensor_tensor(out=t.t[:, 1, :], in0=prt.t[0:64, :], in1=csb.t[:, 1, :], op=ALU.mult), [prt, csb], [t])
                    k.op("dve", lambda e: e.tensor_tensor(out=dst_ap, in0=t.t[:, 0, :], in1=t.t[:, 1, :], op=ALU.add), [t], [dst])

                for c in range(NCH):
                    t0 = c * 512
                    xb = B_y[c] if l > 0 else B_const
                    csb = cst.next()
                    k.dma("sp", csb.t[:], cs_in[:, :, t0:t0 + 512].rearrange("a p t -> p a t"), [B_const], [csb], csb)
                    for tt in range(4):
                        xtile = xt.next(); hbt = hb.next(); ss = ssq.next()
                        k.dma("sp", xtile.t[:], xsrc[t0 + tt * 128:t0 + (tt + 1) * 128, :], [xb], [xtile], xtile)
                        k.op("dve", lambda e: e.memset(ss.t[:], 0.0), [], [ss])
                        k.op("act", lambda e: e.activation(out=hbt.t[:], in_=xtile.t[:], func=AF.Square, accum_out=ss.t[:, 0:1]), [xtile], [hbt, ss])
                        k.op("act", lambda e: e.activation(out=ss.t[:, 1:2], in_=ss.t[:, 0:1], func=AF.Sqrt, bias=EPS, scale=1.0 / D), [ss], [ss])
                        k.op("dve", lambda e: e.reciprocal(out=ss.t[:, 1:2], in_=ss.t[:, 1:2]), [ss], [ss])
                        k.op("dve", lambda e: e.tensor_scalar(out=hbt.t[:], in0=xtile.t[:], scalar1=ss.t[:, 1:2], scalar2=None, op0=ALU.mult), [xtile, ss], [hbt])
                        for half in range(2):
                            for j in range(8):
                                kt = half * 8 + j
                                k.op("pe", lambda e, kt=kt, j=j: e.transpose(PT.t[:, j * 128:(j + 1) * 128], hbt.t[:, kt * 128:(kt + 1) * 128], ident.t[:]), [hbt, ident], [PT])
                            for j in range(8):
                                kt = half * 8 + j
                                k.op("dve", lambda e, kt=kt, j=j: e.tensor_scalar(out=hT.t[:, kt, tt * 128:(tt + 1) * 128], in0=PT.t[:, j * 128:(j + 1) * 128], scalar1=col(kt), scalar2=None, op0=ALU.mult), [PT, cols], [hT])

                    def fm_tile(wblk, m0, mw, rhsbuf, nk=16):
                        p = pmm.next()
                        for kt in range(nk):
                            k.op("pe", lambda e, kt=kt: e.matmul(p.t[0:mw, :], wblk.t[:, kt, m0:m0 + mw], rhsbuf.t[:, kt, :], start=(kt == 0), stop=(kt == nk - 1)), [wblk, rhsbuf], [p])
                        return p

                    wblk = wsl.next(); wblock_load(wblk, "w_in", l, 0, 16, 0, 512)
                    pss = [fm_tile(wblk, m * 128, 128, hT) for m in range(4)]
                    rms_norm_fm(pss, 128, 512, [col(64 + m) for m in range(4)], [(cqn, cqn.t[:, m, :]) for m in range(4)])
                    wblk = wsl.next(); wblock_load(wblk, "w_in", l, 0, 16, 512, 512)
                    pss = [fm_tile(wblk, m * 128, 128, hT) for m in range(4)]
                    rms_norm_fm(pss, 128, 512, [col(68 + m) for m in range(4)], [(ckvn, ckvn.t[:, m, :]) for m in range(4)])
                    wblk = wsl.next(); wblock_load(wblk, "w_in", l, 0, 16, 1024, 64)
                    p = fm_tile(wblk, 0, 64, hT)
                    rb = rpb.next()
                    rms_norm_fm([p], 64, 64, [col(75, 64)], [(rb, rb.t[:])])
                    st = stg.next()
                    rope(rb, csb, st, st.t[0:64, 0:512])
                    store(krT[:, t0:t0 + 512], B_kr, st, 64, 512)
                    for h in range(8):
                        p = fm_tile(wuq, h * 192, 128, cqn, nk=4)
                        st = stg.next()
                        rms_norm_fm([p], 128, 128, [col(72)], [(st, st.t[:, 0:512])])
                        store(qnT[h, :, t0:t0 + 512], B_qn, st, 128, 512)
                        p = fm_tile(wuq, h * 192 + 128, 64, cqn, nk=4)
                        rb = rpb.next()
                        rms_norm_fm([p], 64, 64, [col(73, 64)], [(rb, rb.t[:])])
                        st = stg.next()
                        rope(rb, csb, st, st.t[0:64, 0:512])
                        store(qrT[h, :, t0:t0 + 512], B_qr, st, 64, 512)
                    for h in range(8):
                        p = fm_tile(wukv, h * 256, 128, ckvn, nk=4)
                        st = stg.next()
                        rms_norm_fm([p], 128, 128, [col(74)], [(st, st.t[:, 0:512])])
                        store(knT[h, :, t0:t0 + 512], B_kn, st, 128, 512)
                    wv = wukv.t[:].rearrange("p k (h c) -> p k h c", c=256)
                    for tt in range(4):
                        st = stg.next()
                        for half in range(2):
                            p = pmm.next()
                            for kt in range(4):
                                k.op("pe", lambda e, kt=kt: e.matmul(p.t[:].rearrange("p (h d) -> p h d", d=128), ckvn.t[:, kt, tt * 128:(tt + 1) * 128], wv[:, kt, half * 4:(half + 1) * 4, 128:256], start=(kt == 0), stop=(kt == 3)), [ckvn, wukv], [p])
                            k.op("act", lambda e: e.activation(out=st.t[:, half * 512:(half + 1) * 512], in_=p.t[:], func=AF.Copy), [p], [st])
                        store(vA[t0 + tt * 128:t0 + (tt + 1) * 128, :], B_vA, st, 128, 1024)
                    for bi in range(4):
                        wblk = wsl.next(); wblock_load(wblk, "w_in", l, 0, 16, 1088 + bi * 512, 512)
                        for m in range(4):
                            hh = (bi % 2) * 4 + m
                            p = fm_tile(wblk, m * 128, 128, hT)
                            st = stg.next()
                            rms_norm_fm([p], 128, 128, [col(76 if bi < 2 else 77)], [(st, st.t[:, 0:512])])
                            if bi < 2:
                                store(qdT[hh, :, t0:t0 + 512], B_qd, st, 128, 512)
                            else:
                                store(kdT[hh, :, t0:t0 + 512], B_kd, st, 128, 512)
                    wv0 = wsl.next(); wblock_load(wv0, "w_in", l, 0, 16, 3136, 512)
                    wv1 = wsl.next(); wblock_load(wv1, "w_in", l, 0, 16, 3136 + 512, 512)
                    for tt in range(4):
                        st = stg.next()
                        for half, wvb in enumerate((wv0, wv1)):
                            p = pmm.next()
                            for kt in range(16):
                                k.op("pe", lambda e, kt=kt: e.matmul(p.t[:], hT.t[:, kt, tt * 128:(tt + 1) * 128], wvb.t[:, kt, :], start=(kt == 0), stop=(kt == 15)), [hT, wvb], [p])
                            k.op("act", lambda e: e.activation(out=st.t[:, half * 512:(half + 1) * 512], in_=p.t[:], func=AF.Copy), [p], [st])
                        store(vD[t0 + tt * 128:t0 + (tt + 1) * 128, :], B_vD, st, 128, 1024)
                    for bi in range(8):
                        wblk = wsl.next(); wblock_load(wblk, "w_in", l, 0, 16, 4160 + bi * 512, 512)
                        for m in range(4):
                            mt = bi * 4 + m
                            p = fm_tile(wblk, m * 128, 128, hT)
                            st = stg.next()
                            k.op("act", lambda e: e.activation(out=st.t[:, 0:512], in_=p.t[:], func=AF.Sigmoid, bias=col(32 + mt), scale=1.0), [p, cols], [st])
                            store(gT[mt, :, t0:t0 + 512], B_g, st, 128, 512)
                end_phase()

            for branch in range(2):
                with contextlib.ExitStack() as ph:
                    Kh = sb("b_K", [128, S], BF16, ph)
                    Vh = sb("b_V", [128, NKT, 128], BF16, ph)
                    qn = Rot([sb("b_qn%d" % i, [128, 512], BF16, ph) for i in range(2)])
                    Pb = Rot([sb("b_P%d" % i, [128, 512], BF16, ph) for i in range(4)])
                    rl = Rot([sb("b_rl%d" % i, [128, 512], F32, ph) for i in range(2)])
                    ost = Rot([sb("b_o%d" % i, [128, 512], BF16, ph) for i in range(2)])
                    sps = Rot(PS[0:3])
                    ops = Rot([(PS[3], PS[4]), (PS[5], PS[6])])
                    if branch == 0:
                        Kr = sb("b_Kr", [64, S], BF16, ph)
                        qr = Rot([sb("b_qr%d" % i, [64, 512], BF16, ph) for i in range(2)])
                        for s0 in range(0, S, 2048):
                            s1 = min(S, s0 + 2048)
                            k.dma("sp", Kr.t[:, s0:s1], krT[:, s0:s1], [B_kr], [Kr], Kr)
                        Eb = None
                    else:
                        Wt = sb("b_W", [128, NREL, 512], BF16, ph)
                        Eb = Rot([sb("b_E%d" % i, [128, 512], BF16, ph) for i in range(3)])
                    ksrc, kbuf = (knT, B_kn) if branch == 0 else (kdT, B_kd)
                    vsrc, vbuf = (vA, B_vA) if branch == 0 else (vD, B_vD)
                    qsrc, qbuf = (qnT, B_qn) if branch == 0 else (qdT, B_qd)
                    odst, obuf = (oaT, B_oa) if branch == 0 else (obT, B_ob)
                    for h in range(8):
                        for s0 in range(0, S, 2048):
                            s1 = min(S, s0 + 2048)
                            k.dma("sp", Kh.t[:, s0:s1], ksrc[h, :, s0:s1], [kbuf], [Kh], Kh)
                        for k0 in range(0, NKT, 16):
                            k1 = min(NKT, k0 + 16)
                            k.dma("sp", Vh.t[:, k0:k1, :], vsrc[k0 * 128:k1 * 128, h * 128:(h + 1) * 128].rearrange("(kt p) d -> p kt d", p=128), [vbuf], [Vh], Vh)
                        if branch == 1:
                            for r0 in range(0, NREL, 4):
                                k.dma("sp", Wt.t[:, r0:r0 + 4, :], wtab_in[h, :, r0:r0 + 4, :], [B_const], [Wt], Wt)
                        for qc in range(NCH):
                            q0 = qc * 512
                            qt = qn.next()
                            k.dma("sp", qt.t[:], qsrc[h, :, q0:q0 + 512], [qbuf], [qt], qt)
                            if branch == 0:
                                qrt = qr.next()
                                k.dma("sp", qrt.t[:], qrT[h, :, q0:q0 + 512], [B_qr], [qrt], qrt)
                                kts = list(range(NKT))
                            else:
                                kts = [kt for kt in range(4 * qc - 8, 4 * qc + 12) if 0 <= kt < NKT]
                            Ops, Lps = ops.next()
                            n = len(kts)
                            Ps = [None] * n
                            for step in range(n + 2):
                                if step < n:
                                    kt = kts[step]
                                    sp_ = sps.next()
                                    if branch == 0:
                                        k.op("pe", lambda e: e.matmul(sp_.t[:], Kh.t[:, kt * 128:(kt + 1) * 128], qt.t[:], start=True, stop=False), [Kh, qt], [sp_])
                                        k.op("pe", lambda e: e.matmul(sp_.t[:], Kr.t[:, kt * 128:(kt + 1) * 128], qrt.t[:], start=False, stop=True), [Kr, qrt], [sp_])
                                        P = Pb.next()
                                        k.op("act", lambda e: e.activation(out=P.t[:], in_=sp_.t[:], func=AF.Exp, scale=SC_A), [sp_], [P])
                                    else:
                                        rel = kt - 4 * qc + 8
                                        k.op("pe", lambda e: e.matmul(sp_.t[:], Kh.t[:, kt * 128:(kt + 1) * 128], qt.t[:], start=True, stop=True), [Kh, qt], [sp_])
                                        Et = Eb.next()
                                        k.op("act", lambda e: e.activation(out=Et.t[:], in_=sp_.t[:], func=AF.Exp, scale=SC_D), [sp_], [Et])
                                        P = Pb.next()
                                        k.op("dve", lambda e: e.tensor_tensor(out=P.t[:], in0=Et.t[:], in1=Wt.t[:, rel, :], op=ALU.mult), [Et, Wt], [P])
                                    Ps[step] = P
                                if step >= 2:
                                    i = step - 2
                                    kt = kts[i]
                                    P = Ps[i]
                                    k.op("pe", lambda e: e.matmul(Ops.t[:], Vh.t[:, kt, :], P.t[:], start=(i == 0), stop=(i == n - 1)), [Vh, P], [Ops])
                                    k.op("pe", lambda e: e.matmul(Lps.t[:], ones.t[:], P.t[:], start=(i == 0), stop=(i == n - 1)), [ones, P], [Lps])
                            r = rl.next(); o = ost.next()
                            k.op("dve", lambda e: e.reciprocal(out=r.t[:], in_=Lps.t[:]), [Lps], [r])
                            k.op("dve", lambda e: e.tensor_tensor(out=o.t[:], in0=Ops.t[:], in1=r.t[:], op=ALU.mult), [Ops, r], [o])
                            k.dma("pool", odst[h, :, q0:q0 + 512], o.t[:], [o], [obuf], o)
                    end_phase()

            with contextlib.ExitStack() as ph:
                x1 = [sb("c_x%d" % i, [128, D], F32, ph) for i in range(4)]
                hb = Rot([sb("c_hb%d" % i, [128, D], BF16, ph) for i in range(2)])
                ssq = Rot([sb("c_ss%d" % i, [128, 2], F32, ph) for i in range(2)])
                h2T = sb("c_h2T", [128, 16, 512], BF16, ph)
                actT = sb("c_actT", [128, 44, 512], BF16, ph)
                oa = sb("c_oa", [128, 8, 512], BF16, ph)
                ob_ = sb("c_ob", [128, 8, 512], BF16, ph)
                mT = sb("c_mT", [128, 16, 512], BF16, ph)
                gt_ = Rot([sb("c_g%d" % i, [128, 2, 512], BF16, ph) for i in range(2)])
                tmp = Rot([sb("c_t%d" % i, [128, 2, 512], F32, ph) for i in range(2)])
                wsl = Rot([sb("c_w%d" % i, [128, 16, 512], BF16, ph) for i in range(3)])
                pmm = Rot(PS[0:6])
                for c in range(NCH):
                    t0 = c * 512
                    xb = B_y[c] if l > 0 else B_const
                    for tt in range(4):
                        k.dma("sp", x1[tt].t[:], xsrc[t0 + tt * 128:t0 + (tt + 1) * 128, :], [xb], [x1[tt]], x1[tt])
                    for s0 in range(0, 8, 4):
                        k.dma("sp", oa.t[:, s0:s0 + 4, :], oaT[s0:s0 + 4, :, t0:t0 + 512].rearrange("h p t -> p h t"), [B_oa], [oa], oa)
                        k.dma("sp", ob_.t[:, s0:s0 + 4, :], obT[s0:s0 + 4, :, t0:t0 + 512].rearrange("h p t -> p h t"), [B_ob], [ob_], ob_)
                    for bi in range(4):
                        wa = wsl.next(); wblock_load(wa, "w_o_mla", l, 0, 8, bi * 512, 512)
                        wd = wsl.next(); wblock_load(wd, "w_o_dil", l, 0, 8, bi * 512, 512)
                        for m in range(4):
                            mt = bi * 4 + m
                            g = gt_.next()
                            k.dma("sp", g.t[:, 0, :], gT[mt, :, t0:t0 + 512], [B_g], [g], g)
                            k.dma("sp", g.t[:, 1, :], gT[16 + mt, :, t0:t0 + 512], [B_g], [g], g)
                            pa = pmm.next(); pb = pmm.next()
                            for kt in range(8):
                                k.op("pe", lambda e, kt=kt: e.matmul(pa.t[:], wa.t[:, kt, m * 128:(m + 1) * 128], oa.t[:, kt, :], start=(kt == 0), stop=(kt == 7)), [wa, oa], [pa])
                            for kt in range(8):
                                k.op("pe", lambda e, kt=kt: e.matmul(pb.t[:], wd.t[:, kt, m * 128:(m + 1) * 128], ob_.t[:, kt, :], start=(kt == 0), stop=(kt == 7)), [wd, ob_], [pb])
                            t = tmp.next()
                            k.op("dve", lambda e: e.tensor_tensor(out=t.t[:, 0, :], in0=pa.t[:], in1=g.t[:, 0, :], op=ALU.mult), [pa, g], [t])
                            k.op("dve", lambda e: e.tensor_tensor(out=t.t[:, 1, :], in0=pb.t[:], in1=g.t[:, 1, :], op=ALU.mult), [pb, g], [t])
                            k.op("dve", lambda e: e.tensor_tensor(out=mT.t[:, mt, :], in0=t.t[:, 0, :], in1=t.t[:, 1, :], op=ALU.add), [t], [mT])
                    for cg in range(4):
                        wo = wsl.next(); wblock_load(wo, "w_o", l, 0, 16, cg * 512, 512)
                        for tt in range(4):
                            p = pmm.next()
                            for kt in range(16):
                                k.op("pe", lambda e, kt=kt: e.matmul(p.t[:], mT.t[:, kt, tt * 128:(tt + 1) * 128], wo.t[:, kt, :], start=(kt == 0), stop=(kt == 15)), [mT, wo], [p])
                            k.op("dve", lambda e: e.tensor_tensor(out=x1[tt].t[:, cg * 512:(cg + 1) * 512], in0=p.t[:], in1=x1[tt].t[:, cg * 512:(cg + 1) * 512], op=ALU.add), [p, x1[tt]], [x1[tt]])
                    for tt in range(4):
                        hbt = hb.next(); ss = ssq.next()
                        k.op("dve", lambda e: e.memset(ss.t[:], 0.0), [], [ss])
                        k.op("act", lambda e: e.activation(out=hbt.t[:], in_=x1[tt].t[:], func=AF.Square, accum_out=ss.t[:, 0:1]), [x1[tt]], [hbt, ss])
                        k.op("act", lambda e: e.activation(out=ss.t[:, 1:2], in_=ss.t[:, 0:1], func=AF.Sqrt, bias=EPS, scale=1.0 / D), [ss], [ss])
                        k.op("dve", lambda e: e.reciprocal(out=ss.t[:, 1:2], in_=ss.t[:, 1:2]), [ss], [ss])
                        k.op("dve", lambda e: e.tensor_scalar(out=hbt.t[:], in0=x1[tt].t[:], scalar1=ss.t[:, 1:2], scalar2=None, op0=ALU.mult), [x1[tt], ss], [hbt])
                        for half in range(2):
                            for j in range(8):
                                kt = half * 8 + j
                                k.op("pe", lambda e, kt=kt, j=j: e.transpose(PT.t[:, j * 128:(j + 1) * 128], hbt.t[:, kt * 128:(kt + 1) * 128], ident.t[:]), [hbt, ident], [PT])
                            for j in range(8):
                                kt = half * 8 + j
                                k.op("dve", lambda e, kt=kt, j=j: e.tensor_scalar(out=h2T.t[:, kt, tt * 128:(tt + 1) * 128], in0=PT.t[:, j * 128:(j + 1) * 128], scalar1=col(16 + kt), scalar2=None, op0=ALU.mult), [PT, cols], [h2T])
                    for bi in range(11):
                        wg = wsl.next(); wblock_load(wg, "w_gate_up", l, 0, 16, bi * 512, 512)
                        wu = wsl.next(); wblock_load(wu, "w_gate_up", l, 0, 16, D_FF + bi * 512, 512)
                        for m in range(4):
                            j = bi * 4 + m
                            pg = pmm.next(); pu = pmm.next()
                            for kt in range(16):
                                k.op("pe", lambda e, kt=kt: e.matmul(pg.t[:], wg.t[:, kt, m * 128:(m + 1) * 128], h2T.t[:, kt, :], start=(kt == 0), stop=(kt == 15)), [wg, h2T], [pg])
                            for kt in range(16):
                                k.op("pe", lambda e, kt=kt: e.matmul(pu.t[:], wu.t[:, kt, m * 128:(m + 1) * 128], h2T.t[:, kt, :], start=(kt == 0), stop=(kt == 15)), [wu, h2T], [pu])
                            t = tmp.next()
                            k.op("act", lambda e: e.activation(out=t.t[:, 0, :], in_=pg.t[:], func=AF.Silu), [pg], [t])
                            k.op("dve", lambda e: e.tensor_tensor(out=actT.t[:, j, :], in0=pu.t[:], in1=t.t[:, 0, :], op=ALU.mult), [pu, t], [actT])
                    kparts = [(0, 16), (16, 16), (32, 12)]
                    for cg in range(4):
                        pd = [pmm.next() for _ in range(4)]
                        for (kp0, nk) in kparts:
                            wdn = wsl.next(); wblock_load(wdn, "w_down", l, kp0 * 128, nk, cg * 512, 512)
                            for tt in range(4):
                                for kk in range(nk):
                                    kt = kp0 + kk
                                    k.op("pe", lambda e, kk=kk, kt=kt: e.matmul(pd[tt].t[:], actT.t[:, kt, tt * 128:(tt + 1) * 128], wdn.t[:, kk, :], start=(kt == 0), stop=(kt == 43)), [actT, wdn], [pd[tt]])
                        for tt in range(4):
                            k.op("dve", lambda e: e.tensor_tensor(out=x1[tt].t[:, cg * 512:(cg + 1) * 512], in0=pd[tt].t[:], in1=x1[tt].t[:, cg * 512:(cg + 1) * 512], op=ALU.add), [pd[tt], x1[tt]], [x1[tt]])
                    for tt in range(4):
                        k.dma("pool", y[t0 + tt * 128:t0 + (tt + 1) * 128, :], x1[tt].t[:], [x1[tt]], [B_y[c]], x1[tt])
                end_phase()

        k.final_wait("pool", B_y)
    return nc


def _const_tables(S):
    inv = (10000.0 ** (-np.arange(0, 64, 2, dtype=np.float32) / np.float32(64))).astype(np.float32)
    ang = np.arange(S, dtype=np.float32)[:, None] * inv[None, :]
    cos = np.cos(ang).astype(np.float32).T
    sin = np.sin(ang).astype(np.float32).T
    cs = np.stack([np.concatenate([cos, cos], 0), np.concatenate([sin, sin], 0)], 0)
    i = np.arange(128)[:, None]
    j = np.arange(512)[None, :]
    wt = np.zeros((8, 128, NREL, 512), np.float32)
    for r in range(NREL):
        d = (r - 8) * 128 + i - j
        a = np.abs(d)
        mult = (a <= 64).astype(np.float32) + ((d % 4 == 0) & (a <= 256)) + ((d % 16 == 0) & (a <= 1024))
        for h in range(8):
            slope = 2.0 ** (-8.0 * (h + 1) / 8)
            wt[h, :, r, :] = mult * np.exp(-slope * a)
    ident = np.eye(128, dtype=np.float32)
    rmat = np.zeros((64, 64), np.float32)
    for m in range(32):
        rmat[m + 32, m] = -1.0
        rmat[m, m + 32] = 1.0
    bf = ml_dtypes.bfloat16
    return cs, wt.astype(bf), ident.astype(bf), rmat.astype(bf)


def _pack_cols(inp, depth):
    cols = np.zeros((depth, 128, NCOL), np.float32)
    for l in range(depth):
        cols[l, :, 0:16] = np.asarray(inp["g_mix"][l]).reshape(16, 128).T
        cols[l, :, 16:32] = np.asarray(inp["g_ffn"][l]).reshape(16, 128).T
        cols[l, :, 32:64] = np.asarray(inp["b_gate"][l]).reshape(32, 128).T
        cols[l, :, 64:68] = np.asarray(inp["g_cq"][l]).reshape(4, 128).T
        cols[l, :, 68:72] = np.asarray(inp["g_ckv"][l]).reshape(4, 128).T
        cols[l, :, 72] = np.asarray(inp["g_q_nope"][l])
        cols[l, 0:64, 73] = np.asarray(inp["g_q_rope"][l])
        cols[l, :, 74] = np.asarray(inp["g_k_nope"][l])
        cols[l, 0:64, 75] = np.asarray(inp["g_k_rope"][l])
        cols[l, :, 76] = np.asarray(inp["g_q_dil"][l])
        cols[l, :, 77] = np.asarray(inp["g_k_dil"][l])
    return cols


_NC_CACHE = {}


def run(inputs, S, depth, nb):
    if (S, depth) not in _NC_CACHE:
        _NC_CACHE[(S, depth)] = build(S, depth)
    nc = _NC_CACHE[(S, depth)]
    cs, wt, ident, rmat = _const_tables(S)
    cols = _pack_cols(inputs, depth)
    shared = {n: np.ascontiguousarray(np.asarray(inputs[n], dtype=np.float32)) for n in
              ("w_in", "w_uq", "w_ukv", "w_o_mla", "w_o_dil", "w_o", "w_gate_up", "w_down")}
    x = np.asarray(inputs["x"], dtype=np.float32)
    in_maps = []
    for b in range(nb):
        m = dict(shared)
        m.update(x=np.ascontiguousarray(x[b]), cols=cols, cs=cs, wtab=wt, ident=ident, rmat=rmat)
        in_maps.append(m)
    res = run_bass_kernel_spmd(nc, in_maps, core_ids=list(range(nb)))
    return np.stack([res.results[b]["y"] for b in range(nb)], 0).astype(np.float32)


LAYERS_PER_LAUNCH = 1


def kernel(**inputs):
    x = np.asarray(inputs["x"], dtype=np.float32)
    depth = inputs["w_in"].shape[0]
    names = ("g_mix", "w_in", "b_gate", "g_cq", "w_uq", "g_ckv", "w_ukv", "g_q_nope", "g_q_rope", "g_k_nope",
             "g_k_rope", "g_q_dil", "g_k_dil", "w_o_mla", "w_o_dil", "w_o", "g_ffn", "w_gate_up", "w_down")
    for l0 in range(0, depth, LAYERS_PER_LAUNCH):
        sub = {n: np.asarray(inputs[n])[l0:l0 + LAYERS_PER_LAUNCH] for n in names}
        sub["x"] = x
        x = run(sub, x.shape[1], LAYERS_PER_LAUNCH, x.shape[0])
    return x
```
